# Optimizing a Trainium2 kernel written in Bass

```python
import jax, jax.numpy as jnp
from jax import lax
import numpy as np

D_MODEL = 1024
BATCH = 8
SEQ = 4096
DEPTH = 2

CHUNK = 64
N_MIXERS = 2
N_A = (DEPTH + 1) // 2
N_B = DEPTH // 2
SHORT_CONV_WIDTH = 3
CONFORMER_CONV_WIDTH = 31
D_FF = ((8 * D_MODEL // 3 + 255) // 256) * 256
RMS_EPS = 1e-6
LN_EPS = 1e-5

kernel_name = "hybrid_shortconv_conformer_conv_encoder"


def rms_norm(x, g):
    xf = x.astype(jnp.float32)
    y = xf * lax.rsqrt(jnp.mean(xf * xf, axis=-1, keepdims=True) + RMS_EPS)
    return (y * g.astype(jnp.float32)).astype(x.dtype)


def layer_norm(x, g, b):
    xf = x.astype(jnp.float32)
    mu = jnp.mean(xf, axis=-1, keepdims=True)
    var = jnp.mean(jnp.square(xf - mu), axis=-1, keepdims=True)
    y = (xf - mu) * lax.rsqrt(var + LN_EPS)
    return (y * g.astype(jnp.float32) + b.astype(jnp.float32)).astype(x.dtype)


def causal_depthwise_conv(x, w):
    k = w.shape[0]
    return lax.conv_general_dilated(
        x, w[:, None, :].astype(x.dtype), window_strides=(1,),
        padding=[(k - 1, 0)], dimension_numbers=("NWC", "WIO", "NWC"),
        feature_group_count=x.shape[-1])


def short_gated_conv(h, w_in, w_conv, w_out):
    bcv = jnp.einsum("bsd,de->bse", h, w_in)
    gate_b, gate_c, v = jnp.split(bcv, 3, axis=-1)
    y = gate_b * causal_depthwise_conv(gate_c * v, w_conv)
    return jnp.einsum("bsd,de->bse", y, w_out)


def conformer_conv_module(h, w_pw1, b_pw1, w_dw, b_dw, ln_g, ln_b, w_pw2, b_pw2):
    u = jnp.einsum("bsd,de->bse", h, w_pw1) + b_pw1
    a, g = jnp.split(u, 2, axis=-1)
    u = a * jax.nn.sigmoid(g)
    u = causal_depthwise_conv(u, w_dw) + b_dw
    u = jax.nn.silu(layer_norm(u, ln_g, ln_b))
    return jnp.einsum("bsd,de->bse", u, w_pw2) + b_pw2


def swiglu(h, w_gate, w_up, w_down):
    gu = jax.nn.silu(jnp.einsum("bsd,df->bsf", h, w_gate)) * jnp.einsum("bsd,df->bsf", h, w_up)
    return jnp.einsum("bsf,fd->bsd", gu, w_down)


def setup_inputs(seed: int = 0) -> dict:
    key = jax.random.key(seed)
    ks = jax.random.split(key, 24)
    D, F = D_MODEL, D_FF
    nrm = lambda k, shape, fan_in: jax.random.normal(k, shape, jnp.float32) * (fan_in ** -0.5)
    gain = lambda k, shape: 1.0 + 0.02 * jax.random.normal(k, shape, jnp.float32)
    small = lambda k, shape: 0.02 * jax.random.normal(k, shape, jnp.float32)
    return {
        "x": jax.random.normal(ks[0], (BATCH, SEQ, D), jnp.float32),
        "a_norm": gain(ks[1], (N_A, D)),
        "a_w_in": nrm(ks[2], (N_A, D, 3 * D), D),
        "a_conv": nrm(ks[3], (N_A, SHORT_CONV_WIDTH, D), SHORT_CONV_WIDTH),
        "a_w_out": nrm(ks[4], (N_A, D, D), D),
        "b_norm": gain(ks[5], (N_B, D)),
        "b_w_pw1": nrm(ks[6], (N_B, D, 2 * D), D),
        "b_b_pw1": small(ks[7], (N_B, 2 * D)),
        "b_conv": nrm(ks[8], (N_B, CONFORMER_CONV_WIDTH, D), CONFORMER_CONV_WIDTH),
        "b_b_conv": small(ks[9], (N_B, D)),
        "b_ln_g": gain(ks[10], (N_B, D)),
        "b_ln_b": small(ks[11], (N_B, D)),
        "b_w_pw2": nrm(ks[12], (N_B, D, D), D),
        "b_b_pw2": small(ks[13], (N_B, D)),
        "ffn_norm": gain(ks[14], (DEPTH, D)),
        "ffn_w_gate": nrm(ks[15], (DEPTH, D, F), D),
        "ffn_w_up": nrm(ks[16], (DEPTH, D, F), D),
        "ffn_w_down": nrm(ks[17], (DEPTH, F, D), F),
        "final_norm": gain(ks[18], (D,)),
    }


def reference(x, a_norm, a_w_in, a_conv, a_w_out,
              b_norm, b_w_pw1, b_b_pw1, b_conv, b_b_conv, b_ln_g, b_ln_b, b_w_pw2, b_b_pw2,
              ffn_norm, ffn_w_gate, ffn_w_up, ffn_w_down, final_norm):
    h = x
    for i in range(DEPTH):
        j = i // N_MIXERS
        if i % N_MIXERS == 0:
            h = h + short_gated_conv(rms_norm(h, a_norm[j]), a_w_in[j], a_conv[j], a_w_out[j])
        else:
            h = h + conformer_conv_module(
                rms_norm(h, b_norm[j]), b_w_pw1[j], b_b_pw1[j], b_conv[j], b_b_conv[j],
                b_ln_g[j], b_ln_b[j], b_w_pw2[j], b_b_pw2[j])
        h = h + swiglu(rms_norm(h, ffn_norm[i]), ffn_w_gate[i], ffn_w_up[i], ffn_w_down[i])
    return rms_norm(h, final_norm)
```

```python
from contextlib import ExitStack
import numpy as np
import concourse.bass as bass
import concourse.mybir as mybir
from concourse.bass_utils import run_bass_kernel_spmd

F32 = mybir.dt.float32
BF16 = mybir.dt.bfloat16
AF = mybir.ActivationFunctionType
ALU = mybir.AluOpType

D = 1024
FF = 2816
KC = D // 128
FC = FF // 128
SEQ = 4096
T = 1024
ST = 512
NS = T // ST
K0 = 3
K1 = 31
OFF = 32
RMS_EPS = 1e-6
LN_EPS = 1e-5
NSLOT = 5
SLAB = 3072
HC = FC // 2
NTMP = 5
CBW = OFF + T + 8

V_ANORM = 0
V_BNORM = 8
V_FNORM0 = 16
V_FNORM1 = 24
V_FINAL = 32
V_BPW1 = 40
V_BDW = 56
V_LNG = 64
V_LNB = 72
V_BPW2 = 80
V_ACONV = 88
V_WST = 112
NV = 368


class _Op:
    __slots__ = ("eng", "fn", "deps", "signal", "semkey", "semval", "idx", "is_dma")

    def __init__(self, eng, fn, is_dma, semkey):
        self.eng = eng
        self.fn = fn
        self.deps = []
        self.signal = False
        self.semkey = semkey
        self.semval = None
        self.is_dma = is_dma
        if is_dma:
            self.signal = True


class Sched:
    def __init__(self, nc):
        self.nc = nc
        self.ops = []
        self.last_w = {}
        self.readers = {}

    def add(self, eng, fn, reads=(), writes=(), dma_key=None):
        is_dma = dma_key is not None
        op = _Op(eng, fn, is_dma, ("dma", dma_key) if is_dma else ("eng", eng))
        op.idx = len(self.ops)
        deps = {}
        for k in reads:
            w = self.last_w.get(k)
            if w is not None:
                deps[w.idx] = w
        for k in writes:
            w = self.last_w.get(k)
            if w is not None:
                deps[w.idx] = w
            for r in self.readers.get(k, ()):
                deps[r.idx] = r
        for k in reads:
            self.readers.setdefault(k, []).append(op)
        for k in writes:
            self.last_w[k] = op
            self.readers[k] = []
        for d in deps.values():
            if d is op:
                continue
            if d.eng == "pe" and eng == "pe" and not d.is_dma and not is_dma:
                continue
            op.deps.append(d)
            d.signal = True
        self.ops.append(op)
        return op

    def emit(self, final_wait_ops=(), final_eng="sp"):
        nc = self.nc
        fin = _Op(final_eng, None, False, ("eng", final_eng))
        fin.idx = len(self.ops)
        for d in final_wait_ops:
            fin.deps.append(d)
            d.signal = True
        self.ops.append(fin)
        counts = {}
        for op in self.ops:
            if op.signal:
                inc = 16 if op.is_dma else 1
                counts[op.semkey] = counts.get(op.semkey, 0) + inc
                op.semval = counts[op.semkey]
        per_eng = {}
        for op in self.ops:
            per_eng.setdefault(op.eng, []).append(op)
        with ExitStack() as es:
            sems = {}
            for i, k in enumerate(counts.keys()):
                sems[k] = es.enter_context(nc.semaphore("sem%d" % i))
            block = es.enter_context(nc.Block())

            clocks = {}
            know = {}
            plan = {}
            for op in self.ops:
                K = know.setdefault(op.eng, {})
                if op.is_dma:
                    ck = {}
                    waits = []
                    for d in sorted(op.deps, key=lambda d_: -d_.idx):
                        if K.get(d.semkey, 0) >= d.semval:
                            continue
                        waits.append((d.semkey, d.semval))
                        for k_, v_ in clocks[d.idx].items():
                            if K.get(k_, 0) < v_:
                                K[k_] = v_
                    plan[op.idx] = waits
                    for d in op.deps:
                        for k_, v_ in clocks[d.idx].items():
                            if ck.get(k_, 0) < v_:
                                ck[k_] = v_
                    ck[op.semkey] = op.semval
                    clocks[op.idx] = ck
                    continue
                waits = []
                for d in sorted(op.deps, key=lambda d_: -d_.idx):
                    if K.get(d.semkey, 0) >= d.semval:
                        continue
                    waits.append((d.semkey, d.semval))
                    for k_, v_ in clocks[d.idx].items():
                        if K.get(k_, 0) < v_:
                            K[k_] = v_
                plan[op.idx] = waits
                ck = dict(K)
                if op.signal:
                    ck[op.semkey] = op.semval
                clocks[op.idx] = ck
            self.n_waits = {e: sum(len(plan[o.idx]) for o in ops_) for e, ops_ in per_eng.items()}

            def run(engname):
                def body(eng):
                    for op in per_eng.get(engname, ()):
                        waits = plan[op.idx]
                        if op.fn is None:
                            for k, v in waits:
                                eng.wait_ge(sems[k], v)
                            continue
                        for k, v in waits[:-1]:
                            eng.wait_ge(sems[k], v)
                        ret = op.fn(eng)
                        first, last = ret if isinstance(ret, tuple) else (ret, ret)
                        if waits:
                            k, v = waits[-1]
                            first._wait_ge(sems[k], v)
                        if op.signal:
                            last.then_inc(sems[op.semkey], 16 if op.is_dma else 1)
                return body

            block.tensor(run("pe"))
            block.scalar(run("act"))
            block.vector(run("dve"))
            block.gpsimd(run("pool"))
            block.sync(run("sp"))
        return counts


def build_nc(ntok=SEQ, stop_after=None, dbg=9):
    nmt = ntok // T
    nc = bass.Bass("TRN2", target_bir_lowering=False)
    x_d = nc.dram_tensor("x", [ntok, D], F32, kind="ExternalInput").ap()
    out_d = nc.dram_tensor("out", [ntok, D], F32, kind="ExternalOutput").ap()
    vecs_d = nc.dram_tensor("vecs", [128, NV], F32, kind="ExternalInput").ap()
    ident_d = nc.dram_tensor("ident", [128, 128], F32, kind="ExternalInput").ap()
    sident_d = nc.dram_tensor("sident", [128, 32], F32, kind="ExternalInput").ap()
    w_in_d = nc.dram_tensor("w_in_s", [8, 128, 3072], F32, kind="ExternalInput").ap()
    w_out_d = nc.dram_tensor("w_out_s", [4, 128, 2048], F32, kind="ExternalInput").ap()
    pw1_d = nc.dram_tensor("pw1_s", [8, 128, 2048], F32, kind="ExternalInput").ap()
    pw2_d = nc.dram_tensor("pw2_s", [4, 128, 2048], F32, kind="ExternalInput").ap()
    gu_d = [nc.dram_tensor("gu%d_s" % l, [FC, 128, 2048], F32, kind="ExternalInput").ap() for l in range(2)]
    wd_d = [nc.dram_tensor("wd%d_s" % l, [8, 128, HC * 256], F32, kind="ExternalInput").ap() for l in range(2)]

    A = nc.alloc_sbuf_tensor
    xb = [A("xb0", [128, KC, T], F32), A("xb1", [128, KC, T], F32)]
    xn = A("xn", [128, KC, T], BF16)
    cb = A("convbuf", [128, KC, CBW], BF16)
    cvh = A("cvh", [128, KC, K0 - 1], BF16)
    uh = A("uh", [128, KC, K1 - 1], BF16)
    h = A("h_arena", [128, HC, T], BF16)
    sq = A("sq", [128, KC, T], BF16)
    vbf = A("vbf", [128, KC, T], BF16)
    tmp = A("tmp", [128, NTMP, ST], F32)
    stat = A("stat", [128, 4, ST], F32)
    stw = A("stw", [128, 2, 32 * 32], BF16)
    sti = A("sti", [128, 32], BF16)
    stif = A("stif", [128, 32], F32)
    idf = A("idf", [128, 128], F32)
    idb = A("idb", [128, 128], BF16)
    ones = A("ones", [128, 128], BF16)
    vecs = A("vecs_sb", [128, NV], F32)
    wring = A("wring", [128, NSLOT, SLAB], BF16)
    RBW = CBW
    hflat = h[:, :, :].rearrange("p a b -> p (a b)")
    sq32 = sq.bitcast(F32)
    vbf32 = vbf.bitcast(F32)
    banks = [nc.alloc_psum_tensor("bank%d" % i, [128, 512], F32) for i in range(8)]

    S = Sched(nc)
    st = {"bank": 0, "tmp": 0, "dslot": 0, "cur": 0}

    def nbank():
        b = st["bank"]
        st["bank"] = (b + 1) % 8
        return b

    def ntmp():
        t_ = st["tmp"]
        st["tmp"] = (t_ + 1) % NTMP
        return t_

    def vcol(base, i):
        return vecs[:, base + i:base + i + 1]

    slabs = []

    def ffn_slabs(l):
        for hf in range(2):
            for fl in range(HC):
                slabs.append((gu_d[l][hf * HC + fl], 2048))
            for j in range(4):
                slabs.append((wd_d[l][hf * 4 + j], HC * 256))

    for mt in range(nmt):
        for c in range(8):
            slabs.append((w_in_d[c], 3072))
        for j in range(4):
            slabs.append((w_out_d[j], 2048))
        ffn_slabs(0)
        for c in range(8):
            slabs.append((pw1_d[c], 2048))
        for j in range(4):
            slabs.append((pw2_d[j], 2048))
        ffn_slabs(1)
    ws = {"loaded": 0, "cur": 0}

    def ws_advance(i):
        upto = min(len(slabs), i + NSLOT)
        while ws["loaded"] < upto:
            j = ws["loaded"]
            slot = j % NSLOT
            ap, n = slabs[j]
            S.add("pool", lambda e, slot=slot, ap=ap, n=n: e.dma_start(out=wring[:, slot, 0:n], in_=ap),
                  writes=[("w", slot)], dma_key=("w", slot))
            ws["loaded"] += 1

    def ws_hold(n):
        first = ws["cur"]
        ws_advance(first)
        ws["cur"] += n
        return first, [(first + i) % NSLOT for i in range(n)]

    S.add("sp", lambda e: e.dma_start(out=idf[:], in_=ident_d), writes=["idf"], dma_key="idf")
    S.add("sp", lambda e: e.dma_start(out=vecs[:], in_=vecs_d), writes=["vecs"], dma_key="vecs")
    S.add("dve", lambda e: e.memset(ones[:], 1.0), writes=["ones"])
    S.add("dve", lambda e: e.tensor_copy(out=idb[:], in_=idf[:]), reads=["idf"], writes=["idb"])
    S.add("dve", lambda e: e.memset(cvh[:], 0.0), writes=["cvh"])
    S.add("dve", lambda e: e.memset(uh[:], 0.0), writes=["uh"])
    S.add("dve", lambda e: e.memset(cb[:, :, :], 0.0),
          writes=["cbpad"] + [("cbh", c) for c in range(KC)] + [("cb", c, s_) for c in range(KC) for s_ in range(NS)])
    S.add("sp", lambda e: e.dma_start(out=stif[:], in_=sident_d), writes=["stif"], dma_key="stif")
    S.add("dve", lambda e: e.tensor_copy(out=sti[:], in_=stif[:]), reads=["stif"], writes=["sti"])
    ws_advance(0)

    def tok(s):
        return slice(s * ST, (s + 1) * ST)

    pend = []

    def defer(n, tag, fn):
        pend.append([n, tag, fn])

    def pe_tick():
        due = []
        for p in pend:
            p[0] -= 1
            if p[0] <= 0:
                due.append(p)
        for p in due:
            pend.remove(p)
        for p in due:
            p[2]()

    def need(tag):
        due = [p for p in pend if p[1] == tag]
        for p in due:
            pend.remove(p)
        for p in due:
            p[2]()

    def mm_multi(groups, reads, tick=True):
        def fn(e):
            first = ins = None
            for bank, pairs in groups:
                n = len(pairs)
                for i, (l, r) in enumerate(pairs):
                    ins = e.matmul(banks[bank][:, 0:512], lhsT=l, rhs=r, start=(i == 0), stop=(i == n - 1))
                    if first is None:
                        first = ins
            return first, ins
        S.add("pe", fn, reads=reads, writes=[("ps", g[0]) for g in groups])
        if tick:
            pe_tick()

    def mm_group(bank, pairs, reads, tick=True):
        mm_multi([(bank, pairs)], reads, tick)

    def start_norm(s, gbase, to_x, par=None, delay=1):
        p = st["cur"] if par is None else par
        xs = xb[p]
        dst = xs if to_x else xn
        dkey = (lambda c: ("x", p, s, c)) if to_x else (lambda c: ("xn", s, c))
        for c in range(KC):
            S.add("act", lambda e, c=c: e.activation(out=sq[:, c, tok(s)], in_=xs[:, c, tok(s)], func=AF.Square),
                  reads=[("x", p, s, c)], writes=[("sq", s, c)])

        def rest():
            b = nbank()
            mm_group(b, [(ones[:], sq[:, c, tok(s)]) for c in range(KC)],
                     reads=["ones"] + [("sq", s, c) for c in range(KC)], tick=False)
            S.add("act", lambda e: e.activation(out=stat[:, s, :], in_=banks[b][:, 0:512], func=AF.Ln,
                                                scale=1.0 / D, bias=RMS_EPS),
                  reads=[("ps", b)], writes=[("stat", s)])
            S.add("act", lambda e: e.activation(out=stat[:, s, :], in_=stat[:, s, :], func=AF.Exp, scale=-0.5),
                  reads=[("stat", s)], writes=[("stat", s)])
            for c in range(KC):
                S.add("dve", lambda e, c=c: e.scalar_tensor_tensor(
                    out=dst[:, c, tok(s)], in0=xs[:, c, tok(s)], scalar=vcol(gbase, c), in1=stat[:, s, :],
                    op0=ALU.mult, op1=ALU.mult),
                    reads=[("x", p, s, c), ("stat", s), "vecs"], writes=[dkey(c)])
        defer(delay, ("norm", s), rest)

    def resid_add(bank, s, m, bias_col=None):
        p = st["cur"]
        xs = xb[p]
        if bias_col is None:
            S.add("dve", lambda e: e.tensor_tensor(out=xs[:, m, tok(s)], in0=banks[bank][:, 0:512],
                                                   in1=xs[:, m, tok(s)], op=ALU.add),
                  reads=[("ps", bank), ("x", p, s, m)], writes=[("x", p, s, m)])
        else:
            S.add("dve", lambda e: e.scalar_tensor_tensor(out=xs[:, m, tok(s)], in0=banks[bank][:, 0:512],
                                                          scalar=bias_col, in1=xs[:, m, tok(s)],
                                                          op0=ALU.add, op1=ALU.add),
                  reads=[("ps", bank), ("x", p, s, m), "vecs"], writes=[("x", p, s, m)])

    def proj_out(src, skey, nk, nxt, bias_base=None, kouter0=False):
        first, slots = ws_hold(4)
        for s in range(NS):
            if kouter0 and s == 0:
                for m0 in (0, 4):
                    bks = [nbank() for _ in range(4)]
                    for kc in range(nk):
                        def fn(e, kc=kc, m0=m0, bks=bks, s=s):
                            first_ = ins = None
                            for i in range(4):
                                m = m0 + i
                                slot, mm = slots[m // 2], m % 2
                                ins = e.matmul(banks[bks[i]][:, 0:512],
                                               lhsT=wring[:, slot, kc * 256 + mm * 128: kc * 256 + mm * 128 + 128],
                                               rhs=src[:, kc, tok(s)], start=(kc == 0), stop=(kc == nk - 1))
                                if first_ is None:
                                    first_ = ins
                            return first_, ins
                        S.add("pe", fn, reads=[("w", sl) for sl in slots] + [(skey, s, kc)],
                              writes=[("ps", b) for b in bks])
                        pe_tick()
                    for i in range(4):
                        m = m0 + i
                        resid_add(bks[i], s, m, None if bias_base is None else vcol(bias_base, m))
                if nxt is not None:
                    start_norm(s, *nxt)
                continue
            for j in range(4):
                slot = slots[j]
                groups = []
                for mm in range(2):
                    b = nbank()
                    groups.append((b, [(wring[:, slot, kc * 256 + mm * 128: kc * 256 + mm * 128 + 128],
                                        src[:, kc, tok(s)]) for kc in range(nk)]))
                mm_multi(groups, [("w", slot)] + [(skey, s, kc) for kc in range(nk)])
                for mm in range(2):
                    m = 2 * j + mm
                    resid_add(groups[mm][0], s, m, None if bias_base is None else vcol(bias_base, m))
            if nxt is not None:
                start_norm(s, *nxt)
        ws_advance(first + 4)

    def blocks(n, g=3):
        out, i = [], 0
        while i < n:
            out.append(list(range(i, min(n, i + g))))
            i += g
        return out

    def ffn(l, nxt, next_in=None):
        for hf in range(2):
            for bix, blk in enumerate(blocks(HC)):
                if next_in is not None and hf == 0 and bix == 1:
                    next_in("schedule")
                first, slots = ws_hold(len(blk))
                for s in range(NS):
                    need(("norm", s))
                    for bi, fl in enumerate(blk):
                        slot = slots[bi]
                        bg, bu = nbank(), nbank()
                        groups = [(b, [(wring[:, slot, (kc * 2 + j) * 128:(kc * 2 + j + 1) * 128], xn[:, kc, tok(s)])
                                       for kc in range(KC)]) for j, b in ((0, bg), (1, bu))]
                        mm_multi(groups, [("w", slot)] + [("xn", s, kc) for kc in range(KC)])
                        t_ = ntmp()
                        S.add("act", lambda e, t_=t_, bg=bg: e.activation(out=tmp[:, t_, :], in_=banks[bg][:, 0:512],
                                                                          func=AF.Silu),
                              reads=[("ps", bg)], writes=[("tmp", t_)])
                        S.add("dve", lambda e, t_=t_, bu=bu, fl=fl, s=s: e.tensor_tensor(
                            out=h[:, fl, tok(s)], in0=banks[bu][:, 0:512], in1=tmp[:, t_, :], op=ALU.mult),
                            reads=[("ps", bu), ("tmp", t_)], writes=[("h", s, fl)])
                ws_advance(first + len(blk))
            if next_in is not None and hf == 1:
                next_in("finish")
            proj_out(h, "h", HC, nxt if hf == 1 else None)

    def mixer0(nxt):
        S.add("pool", lambda e: e.tensor_copy(out=cb[:, :, OFF - (K0 - 1):OFF], in_=cvh[:, :, :]),
              reads=["cvh"], writes=[("cbh", c) for c in range(KC)])
        for blk in blocks(KC):
            first, slots = ws_hold(len(blk))
            for s in range(NS):
                need(("norm", s))
                for bi, c in enumerate(blk):
                    slot = slots[bi]
                    bB, bC, bV = nbank(), nbank(), nbank()
                    if bi == 0 and blk[0] == 0:
                        for kc in range(KC):
                            def fn(e, kc=kc, slot=slot, s=s, bb=(bC, bV, bB)):
                                first_ = ins = None
                                for j, b in ((1, bb[0]), (2, bb[1]), (0, bb[2])):
                                    ins = e.matmul(banks[b][:, 0:512],
                                                   lhsT=wring[:, slot, (kc * 3 + j) * 128:(kc * 3 + j + 1) * 128],
                                                   rhs=xn[:, kc, tok(s)], start=(kc == 0), stop=(kc == KC - 1))
                                    if first_ is None:
                                        first_ = ins
                                return first_, ins
                            S.add("pe", fn, reads=[("w", slot), ("xn", s, kc)],
                                  writes=[("ps", bC), ("ps", bV), ("ps", bB)])
                        pe_tick()
                    else:
                        groups = [(b, [(wring[:, slot, (kc * 3 + j) * 128:(kc * 3 + j + 1) * 128], xn[:, kc, tok(s)])
                                       for kc in range(KC)]) for j, b in ((1, bC), (2, bV), (0, bB))]
                        mm_multi(groups, [("w", slot)] + [("xn", s, kc) for kc in range(KC)])
                    tc_, tb_, ta_ = ntmp(), ntmp(), ntmp()
                    S.add("act", lambda e, tc_=tc_, bC=bC: e.activation(out=tmp[:, tc_, :], in_=banks[bC][:, 0:512],
                                                                        func=AF.Copy),
                          reads=[("ps", bC)], writes=[("tmp", tc_)])
                    S.add("act", lambda e, tb_=tb_, bB=bB: e.activation(out=tmp[:, tb_, :], in_=banks[bB][:, 0:512],
                                                                        func=AF.Copy),
                          reads=[("ps", bB)], writes=[("tmp", tb_)])
                    S.add("dve", lambda e, tc_=tc_, bV=bV, c=c, s=s: e.tensor_tensor(
                        out=cb[:, c, OFF + s * ST:OFF + (s + 1) * ST], in0=banks[bV][:, 0:512], in1=tmp[:, tc_, :],
                        op=ALU.mult),
                        reads=[("ps", bV), ("tmp", tc_)], writes=[("cb", c, s)])
                    t0 = OFF + s * ST - (K0 - 1)
                    creads = [("cb", c, s), "vecs"] + ([("cbh", c)] if s == 0 else [("cb", c, 0)])
                    S.add("dve", lambda e, ta_=ta_, c=c, t0=t0: e.tensor_scalar(
                        out=tmp[:, ta_, :], in0=cb[:, c, t0:t0 + ST], scalar1=vcol(V_ACONV, c), scalar2=None,
                        op0=ALU.mult), reads=creads, writes=[("tmp", ta_)])
                    for k in (1, 2):
                        S.add("dve", lambda e, ta_=ta_, c=c, t0=t0, k=k: e.scalar_tensor_tensor(
                            out=tmp[:, ta_, :], in0=cb[:, c, t0 + k:t0 + k + ST], scalar=vcol(V_ACONV, k * 8 + c),
                            in1=tmp[:, ta_, :], op0=ALU.mult, op1=ALU.add),
                            reads=creads + [("tmp", ta_)], writes=[("tmp", ta_)])
                    S.add("dve", lambda e, ta_=ta_, tb_=tb_, c=c, s=s: e.tensor_tensor(
                        out=h[:, c, tok(s)], in0=tmp[:, ta_, :], in1=tmp[:, tb_, :], op=ALU.mult),
                        reads=[("tmp", ta_), ("tmp", tb_)], writes=[("h", s, c)])
            ws_advance(first + len(blk))
        S.add("pool", lambda e: e.tensor_copy(out=cvh[:, :, :], in_=cb[:, :, OFF + T - (K0 - 1):OFF + T]),
              reads=[("cb", c, 1) for c in range(KC)], writes=["cvh"])
        proj_out(h, "h", KC, nxt)

    def mixer1(nxt):
        pv = 1 - st["cur"]
        vf = xb[pv]
        S.add("pool", lambda e: e.tensor_copy(out=cb[:, :, OFF - (K1 - 1):OFF], in_=uh[:, :, :]),
              reads=["uh"], writes=[("cbh", c) for c in range(KC)])

        RB0 = [0, 5 * T]

        def rb_keys(buf):
            lo, hi = RB0[buf], RB0[buf] + 4 * RBW - 1
            return [("h", s_, fl) for fl in range(lo // T, hi // T + 1) for s_ in range(NS)]

        def replicate(c):
            buf = c % 2
            for g in range(4):
                for j in range(4):
                    o0 = RB0[buf] + g * RBW
                    q_ = "sp" if g < (2 if c in (0, KC - 1) else 3) else "act"
                    S.add(q_, lambda e, g=g, j=j, o0=o0: e.dma_start(
                        out=hflat[32 * j:32 * j + 32, o0:o0 + RBW - 4], in_=cb[32 * g:32 * g + 32, c, j:j + RBW - 4]),
                        reads=[("cb", c, 0), ("cb", c, 1), ("cbh", c), "cbpad"],
                        writes=[("rb", buf, g, j)] + (rb_keys(buf) if (g == 0 and j == 0) else []),
                        dma_key=("rb", buf, q_))

        def build_stw(c):
            buf = c % 2

            def fn(e):
                first = ins = None
                for g in range(4):
                    for k0 in range(8):
                        i = g * 8 + k0
                        ins = e.tensor_scalar(out=stw[:, buf, i * 32:(i + 1) * 32], in0=sti[:],
                                              scalar1=vcol(V_WST, (c * 4 + g) * 8 + k0), scalar2=None, op0=ALU.mult)
                        if first is None:
                            first = ins
                return first, ins
            S.add("dve", fn, reads=["sti", "vecs"], writes=[("stw", buf)])

        def conv_pe(c, s):
            buf = c % 2
            bV = nbank()
            base = OFF + s * ST - (K1 - 1)

            def fn(e):
                first = ins = None
                for k0 in range(8):
                    for g in range(4):
                        i = g * 8 + k0
                        o0 = RB0[buf] + g * RBW + base + 4 * k0
                        ins = e.matmul(banks[bV][32 * g:32 * g + 32, 0:512], lhsT=stw[:, buf, i * 32:(i + 1) * 32],
                                       rhs=hflat[:, o0:o0 + ST], start=(k0 == 0), stop=(k0 == 7),
                                       tile_position=(0, 32 * g))
                        if first is None:
                            first = ins
                return first, ins
            S.add("pe", fn, reads=[("rb", buf, g, j) for g in range(4) for j in range(4)] + [("stw", buf)] + rb_keys(buf),
                  writes=[("ps", bV)])
            pe_tick()
            return bV

        def conv_evac(bV, c, s):
            S.add("act", lambda e: e.activation(out=vbf[:, c, tok(s)], in_=banks[bV][:, 0:512],
                                                func=AF.Identity, bias=vcol(V_BDW, c)),
                  reads=[("ps", bV), "vecs"], writes=[("vbf", s, c)])
            S.add("act", lambda e: e.activation(out=sq[:, c, tok(s)], in_=banks[bV][:, 0:512],
                                                func=AF.Square, bias=vcol(V_BDW, c)),
                  reads=[("ps", bV), "vecs"], writes=[("sq", s, c)])
            S.add("act", lambda e: e.activation(out=vf[:, c, tok(s)], in_=banks[bV][:, 0:512],
                                                func=AF.Identity, bias=vcol(V_BDW, c)),
                  reads=[("ps", bV), "vecs"], writes=[("x", pv, s, c)])

        def pw1_item(c, s, slot):
            bA, bG = nbank(), nbank()
            groups = [(b, [(wring[:, slot, (kc * 2 + j) * 128:(kc * 2 + j + 1) * 128], xn[:, kc, tok(s)])
                           for kc in range(KC)]) for j, b in ((1, bG), (0, bA))]
            mm_multi(groups, [("w", slot)] + [("xn", s, kc) for kc in range(KC)])
            t_ = ntmp()
            S.add("act", lambda e: e.activation(
                out=tmp[:, t_, :], in_=banks[bG][:, 0:512], func=AF.Sigmoid, bias=vcol(V_BPW1, 8 + c)),
                reads=[("ps", bG), "vecs"], writes=[("tmp", t_)])
            S.add("dve", lambda e: e.scalar_tensor_tensor(
                out=cb[:, c, OFF + s * ST:OFF + (s + 1) * ST], in0=banks[bA][:, 0:512],
                scalar=vcol(V_BPW1, c), in1=tmp[:, t_, :], op0=ALU.add, op1=ALU.mult),
                reads=[("ps", bA), ("tmp", t_), "vecs"], writes=[("cb", c, s)])

        def conv_chunk(c):
            for s in range(NS):
                bV = conv_pe(c, s)
                conv_evac(bV, c, s)

        full_mix = dbg >= 2
        first, slots = ws_hold(2)
        for s in range(NS):
            need(("norm", s))
            for c in (0, 1):
                pw1_item(c, s, slots[c])
                if full_mix and s == NS - 1 and c == 0:
                    replicate(c)
        ws_advance(first + 2)
        if full_mix:
            build_stw(0)
        for c in range(2, KC):
            first, slots = ws_hold(1)
            for s in range(NS):
                pw1_item(c, s, slots[0])
            ws_advance(first + 1)
            if full_mix:
                if c == 2:
                    replicate(1)
                conv_chunk(c - 2)
                replicate(c)
                build_stw(c - 1)
        if not full_mix:
            return
        conv_chunk(KC - 2)
        build_stw(KC - 1)
        conv_chunk(KC - 1)

        def ln_stats(s):
            b1, b2 = nbank(), nbank()
            mm_group(b1, [(ones[:], vbf[:, c, tok(s)]) for c in range(KC)],
                     ["ones"] + [("vbf", s, c) for c in range(KC)], tick=False)
            mm_group(b2, [(ones[:], sq[:, c, tok(s)]) for c in range(KC)],
                     ["ones"] + [("sq", s, c) for c in range(KC)], tick=False)
            return b1, b2

        def ln_chain(s, b1, b2, eng):
            m_, r_ = 2 - 2 * s, 3 - 2 * s
            S.add("act", lambda e: e.activation(out=stat[:, m_, :], in_=banks[b1][:, 0:512],
                                                func=AF.Copy, scale=1.0 / D),
                  reads=[("ps", b1)], writes=[("stat", m_)])
            S.add("act", lambda e: e.activation(out=stat[:, r_, :], in_=banks[b1][:, 0:512],
                                                func=AF.Square, scale=1.0 / D),
                  reads=[("ps", b1)], writes=[("stat", r_)])
            S.add("dve", lambda e: e.scalar_tensor_tensor(out=stat[:, r_, :], in0=banks[b2][:, 0:512],
                                                          scalar=1.0 / D, in1=stat[:, r_, :],
                                                          op0=ALU.mult, op1=ALU.subtract),
                  reads=[("ps", b2), ("stat", r_)], writes=[("stat", r_)])
            S.add("act", lambda e: e.activation(out=stat[:, r_, :], in_=stat[:, r_, :], func=AF.Ln,
                                                scale=1.0, bias=LN_EPS),
                  reads=[("stat", r_)], writes=[("stat", r_)])
            S.add("act", lambda e: e.activation(out=stat[:, r_, :], in_=stat[:, r_, :], func=AF.Exp, scale=-0.5),
                  reads=[("stat", r_)], writes=[("stat", r_)])
            S.add(eng, lambda e: e.tensor_tensor(out=stat[:, m_, :], in0=stat[:, m_, :], in1=stat[:, r_, :],
                                                 op=ALU.mult),
                  reads=[("stat", m_), ("stat", r_)], writes=[("stat", m_)])

        def ln_apply(c, s):
            m_, r_ = 2 - 2 * s, 3 - 2 * s
            S.add("dve", lambda e: e.tensor_tensor(out=vf[:, c, tok(s)], in0=vf[:, c, tok(s)], in1=stat[:, r_, :],
                                                   op=ALU.mult),
                  reads=[("x", pv, s, c), ("stat", r_)], writes=[("x", pv, s, c)])
            S.add("dve", lambda e: e.tensor_tensor(out=vf[:, c, tok(s)], in0=vf[:, c, tok(s)], in1=stat[:, m_, :],
                                                   op=ALU.subtract),
                  reads=[("x", pv, s, c), ("stat", m_)], writes=[("x", pv, s, c)])
            S.add("act", lambda e: e.activation(out=xn[:, c, tok(s)], in_=vf[:, c, tok(s)], func=AF.Silu,
                                                scale=vcol(V_LNG, c), bias=vcol(V_LNB, c)),
                  reads=[("x", pv, s, c), "vecs"], writes=[("xn", s, c)])

        S.add("pool", lambda e: e.tensor_copy(out=uh[:, :, :], in_=cb[:, :, OFF + T - (K1 - 1):OFF + T]),
              reads=[("cb", c, 1) for c in range(KC)], writes=["uh"])
        st0 = ln_stats(0)
        st1 = ln_stats(1)
        ln_chain(0, *st0, eng="dve")
        ln_chain(1, *st1, eng="pool")
        for s in range(NS):
            for c in range(KC):
                ln_apply(c, s)
        proj_out(xn, "xn", KC, nxt, bias_base=V_BPW2, kouter0=True)

    NSIN = 4
    NSOUT = 4
    in_issued = set()

    def stage_of(mt, tb):
        return (vbf32, "vbf") if (mt == 0 and tb >= NSIN) else (sq32, "sq")

    def issue_input(mt, tb):
        if (mt, tb) in in_issued or mt >= nmt or tb >= T // 128:
            return
        in_issued.add((mt, tb))
        sl_ = tb % NSIN
        r0 = mt * T + tb * 128
        stg, sk = stage_of(mt, tb)
        S.add("sp", lambda e: e.dma_start(out=stg[:, 2 * sl_:2 * sl_ + 2, :],
                                          in_=x_d[r0:r0 + 128, :].rearrange("p (a b) -> p a b", a=2)),
              writes=[(sk, s_, 2 * sl_ + a_) for s_ in range(NS) for a_ in range(2)], dma_key=("sin", sk, sl_))

    def load_tb(mt, tb, par, nxt, tick):
        sl_ = tb % NSIN
        s = tb // 4
        xs = xb[par]
        issue_input(mt, tb)
        stg, sk = stage_of(mt, tb)
        for half in range(2):
            b = nbank()
            ch = 2 * sl_ + half

            def fn(e, b=b, ch=ch):
                first = ins = None
                for q in range(4):
                    ins = e.transpose(out=banks[b][:, q * 128:(q + 1) * 128],
                                      in_=stg[:, ch, q * 128:(q + 1) * 128], identity=idf[:])
                    if first is None:
                        first = ins
                return first, ins
            S.add("pe", fn, reads=[(sk, s_, ch) for s_ in range(NS)] + ["idf"], writes=[("ps", b)])
            if tick:
                pe_tick()
            c0 = half * 4
            t0 = tb * 128
            src = banks[b][:, 0:512].rearrange("p (q t) -> p q t", q=4)
            wk = [("x", par, s, c0 + q) for q in range(4)]
            if half == 0:
                S.add("dve", lambda e, c0=c0, t0=t0, src=src: e.tensor_copy(
                    out=xs[:, c0:c0 + 4, t0:t0 + 128], in_=src), reads=[("ps", b)], writes=wk)
            else:
                S.add("act", lambda e, c0=c0, t0=t0, src=src: e.activation(
                    out=xs[:, c0:c0 + 4, t0:t0 + 128], in_=src, func=AF.Copy), reads=[("ps", b)], writes=wk)
        issue_input(mt, tb + NSIN)
        if tb % 4 == 3 and nxt is not None:
            start_norm(s, *nxt, par=par)

    out_ops = {}

    def store_tb(mt, tb, par, tick):
        sl_ = tb % NSOUT
        s = tb // 4
        xs = xb[par]
        need(("norm", s))
        r0 = mt * T + tb * 128
        for half in range(2):
            b = nbank()

            def fn(e, b=b, half=half):
                first = ins = None
                for q in range(4):
                    c = half * 4 + q
                    ins = e.transpose(out=banks[b][:, q * 128:(q + 1) * 128],
                                      in_=xs[:, c, tb * 128:(tb + 1) * 128], identity=idf[:])
                    if first is None:
                        first = ins
                return first, ins
            S.add("pe", fn, reads=[("x", par, s, half * 4 + q) for q in range(4)] + ["idf"], writes=[("ps", b)])
            if tick:
                pe_tick()
            ch = 2 * sl_ + half
            S.add("act", lambda e, b=b, ch=ch: e.activation(
                out=vbf32[:, ch, :], in_=banks[b][:, 0:512], func=AF.Copy),
                reads=[("ps", b)], writes=[("vbf", 0, ch), ("vbf", 1, ch)])
        hk = [("vbf", s_, 2 * sl_ + hf_) for s_ in range(2) for hf_ in range(2)]
        out_ops[sl_] = S.add("sp", lambda e: e.dma_start(
            out=out_d[r0:r0 + 128, :].rearrange("p (a b) -> p a b", a=2), in_=vbf32[:, 2 * sl_:2 * sl_ + 2, :]),
            reads=hk, dma_key=("sout", sl_))

    def flush(tag):
        need(tag)

    N_A = (V_ANORM, False)
    N_B = (V_BNORM, False)
    N_F0 = (V_FNORM0, False)
    N_F1 = (V_FNORM1, False)
    N_FIN = (V_FINAL, True)
    order = ["in", "l0m", "l0f", "l1m", "l1f"]
    nph = len(order) if stop_after is None else order.index(stop_after) + 1
    n_slab_ph = {"in": 0, "l0m": 12, "l0f": 30, "l1m": 12, "l1f": 30}
    full = stop_after is None
    NTB = T // 128
    if full:
        for tb in range(NTB):
            issue_input(0, tb)
        for tb in range(NTB):
            load_tb(0, tb, 0, N_A, True)
        for mt in range(nmt):
            cur = mt % 2
            st["cur"] = cur
            mixer0(N_F0)
            flush(("out",))
            ffn(0, N_B)
            mixer1(N_F1)

            def next_in(what, mt=mt, cur=cur):
                if mt + 1 >= nmt:
                    return
                if what == "schedule":
                    for tb in range(NSIN):
                        issue_input(mt + 1, tb)
                    for tb in range(NTB):
                        defer(2 + 2 * tb, ("in",), lambda tb=tb: load_tb(mt + 1, tb, 1 - cur, None, False))
                else:
                    flush(("in",))
                    for s_ in range(NS):
                        start_norm(s_, V_ANORM, False, par=1 - cur)
            ffn(1, N_FIN, next_in)
            for tb in range(NTB):
                if mt + 1 < nmt:
                    defer(2 + 2 * tb, ("out",), lambda tb=tb, mt=mt, cur=cur: store_tb(mt, tb, cur, False))
                else:
                    store_tb(mt, tb, cur, True)
    else:
        for mt in range(nmt):
            st["cur"] = 0
            for tb in range(NTB):
                load_tb(mt, tb, 0, None, True)
            if nph > 1:
                for s_ in range(NS):
                    start_norm(s_, *N_A, par=0)
            for ph in order[:nph]:
                if ph == "l0m":
                    mixer0(N_F0 if nph > 2 else None)
                elif ph == "l0f":
                    ffn(0, N_B if nph > 3 else None)
                elif ph == "l1m":
                    mixer1(N_F1 if nph > 4 else None)
                elif ph == "l1f":
                    ffn(1, None)
            for ph in order[nph:]:
                ws["cur"] += n_slab_ph[ph]
            for tb in range(NTB):
                store_tb(mt, tb, 0, True)
    assert not pend
    S.emit(final_wait_ops=list(out_ops.values()))
    return nc


def _fm(v):
    v = np.asarray(v, dtype=np.float32)
    return np.ascontiguousarray(v.reshape(-1, 128).T)


def prep_weights(a_norm, a_w_in, a_conv, a_w_out, b_norm, b_w_pw1, b_b_pw1, b_conv, b_b_conv, b_ln_g, b_ln_b,
                 b_w_pw2, b_b_pw2, ffn_norm, ffn_w_gate, ffn_w_up, ffn_w_down, final_norm):
    f32 = lambda a: np.asarray(a, dtype=np.float32)
    vecs = np.zeros((128, NV), np.float32)
    vecs[:, V_ANORM:V_ANORM + 8] = _fm(f32(a_norm)[0])
    vecs[:, V_BNORM:V_BNORM + 8] = _fm(f32(b_norm)[0])
    vecs[:, V_FNORM0:V_FNORM0 + 8] = _fm(f32(ffn_norm)[0])
    vecs[:, V_FNORM1:V_FNORM1 + 8] = _fm(f32(ffn_norm)[1])
    vecs[:, V_FINAL:V_FINAL + 8] = _fm(f32(final_norm))
    vecs[:, V_BPW1:V_BPW1 + 16] = _fm(f32(b_b_pw1)[0])
    vecs[:, V_BDW:V_BDW + 8] = _fm(f32(b_b_conv)[0])
    vecs[:, V_LNG:V_LNG + 8] = _fm(f32(b_ln_g)[0])
    vecs[:, V_LNB:V_LNB + 8] = _fm(f32(b_ln_b)[0])
    vecs[:, V_BPW2:V_BPW2 + 8] = _fm(f32(b_b_pw2)[0])
    vecs[:, V_ACONV:V_ACONV + 24] = _fm(f32(a_conv)[0].reshape(-1))
    bc = np.concatenate([f32(b_conv)[0], np.zeros((1, D), np.float32)], axis=0)
    bc = bc.reshape(8, 4, 8, 4, 32)
    vecs[:, V_WST:V_WST + 256] = np.ascontiguousarray(bc.transpose(1, 4, 2, 3, 0)).reshape(128, 256)

    def colgroup(w, ncol_groups, j_parts, gcols):
        din = w.shape[0]
        kc = din // 128
        w5 = w.reshape(kc, 128, j_parts, ncol_groups, gcols)
        return np.ascontiguousarray(w5.transpose(3, 1, 0, 2, 4)).reshape(ncol_groups, 128, kc * j_parts * gcols)

    res = {
        "vecs": vecs,
        "ident": np.eye(128, dtype=np.float32),
        "sident": np.tile(np.eye(32, dtype=np.float32), (4, 1)),
        "w_in_s": colgroup(f32(a_w_in)[0], 8, 3, 128),
        "w_out_s": colgroup(f32(a_w_out)[0], 4, 1, 256),
        "pw1_s": colgroup(f32(b_w_pw1)[0], 8, 2, 128),
        "pw2_s": colgroup(f32(b_w_pw2)[0], 4, 1, 256),
    }
    for l in range(2):
        gu = np.concatenate([f32(ffn_w_gate)[l], f32(ffn_w_up)[l]], axis=1)
        res["gu%d_s" % l] = colgroup(gu, FC, 2, 128)
        wdl = f32(ffn_w_down)[l]
        res["wd%d_s" % l] = np.concatenate([colgroup(wdl[hf * HC * 128:(hf + 1) * HC * 128], 4, 1, 256)
                                            for hf in range(2)], axis=0)
    return res


_NC_CACHE = {}


def kernel(x, **weights):
    x = np.asarray(x, dtype=np.float32)
    wmap = prep_weights(**weights)
    nb = x.shape[0]
    if "nc" not in _NC_CACHE:
        _NC_CACHE["nc"] = build_nc(SEQ)
    nc = _NC_CACHE["nc"]
    in_maps = []
    for b in range(nb):
        m = dict(wmap)
        m["x"] = np.ascontiguousarray(x[b])
        in_maps.append(m)
    res = run_bass_kernel_spmd(nc, in_maps, core_ids=list(range(nb)))
    return np.stack([np.asarray(r["out"], dtype=np.float32) for r in res.results], axis=0)
```

```python
from contextlib import ExitStack
import numpy as np
import concourse.bass as bass
import concourse.mybir as mybir
from concourse.bass_utils import run_bass_kernel_spmd

F32 = mybir.dt.float32
BF16 = mybir.dt.bfloat16
AF = mybir.ActivationFunctionType
ALU = mybir.AluOpType

D = 1024
FF = 2816
KC = D // 128
FC = FF // 128
SEQ = 4096
T = 1024
ST = 512
NS = T // ST
K0 = 3
K1 = 31
OFF = 32
RMS_EPS = 1e-6
LN_EPS = 1e-5
NSLOT = 5
SLAB = 3072
HC = FC // 2
NTMP = 5
CBW = OFF + T + 8

V_ANORM = 0
V_BNORM = 8
V_FNORM0 = 16
V_FNORM1 = 24
V_FINAL = 32
V_BPW1 = 40
V_BDW = 56
V_LNG = 64
V_LNB = 72
V_BPW2 = 80
V_ACONV = 88
V_WST = 112
NV = 368


class _Op:
    __slots__ = ("eng", "fn", "deps", "signal", "semkey", "semval", "idx", "is_dma")

    def __init__(self, eng, fn, is_dma, semkey):
        self.eng = eng
        self.fn = fn
        self.deps = []
        self.signal = False
        self.semkey = semkey
        self.semval = None
        self.is_dma = is_dma
        if is_dma:
            self.signal = True


class Sched:
    def __init__(self, nc):
        self.nc = nc
        self.ops = []
        self.last_w = {}
        self.readers = {}

    def add(self, eng, fn, reads=(), writes=(), dma_key=None):
        is_dma = dma_key is not None
        op = _Op(eng, fn, is_dma, ("dma", dma_key) if is_dma else ("eng", eng))
        op.idx = len(self.ops)
        deps = {}
        for k in reads:
            w = self.last_w.get(k)
            if w is not None:
                deps[w.idx] = w
        for k in writes:
            w = self.last_w.get(k)
            if w is not None:
                deps[w.idx] = w
            for r in self.readers.get(k, ()):
                deps[r.idx] = r
        for k in reads:
            self.readers.setdefault(k, []).append(op)
        for k in writes:
            self.last_w[k] = op
            self.readers[k] = []
        for d in deps.values():
            if d is op:
                continue
            if d.eng == "pe" and eng == "pe" and not d.is_dma and not is_dma:
                continue
            op.deps.append(d)
            d.signal = True
        self.ops.append(op)
        return op

    def emit(self, final_wait_ops=(), final_eng="sp"):
        nc = self.nc
        fin = _Op(final_eng, None, False, ("eng", final_eng))
        fin.idx = len(self.ops)
        for d in final_wait_ops:
            fin.deps.append(d)
            d.signal = True
        self.ops.append(fin)
        counts = {}
        for op in self.ops:
            if op.signal:
                inc = 16 if op.is_dma else 1
                counts[op.semkey] = counts.get(op.semkey, 0) + inc
                op.semval = counts[op.semkey]
        per_eng = {}
        for op in self.ops:
            per_eng.setdefault(op.eng, []).append(op)
        with ExitStack() as es:
            sems = {}
            for i, k in enumerate(counts.keys()):
                sems[k] = es.enter_context(nc.semaphore("sem%d" % i))
            block = es.enter_context(nc.Block())

            clocks = {}
            know = {}
            plan = {}
            for op in self.ops:
                K = know.setdefault(op.eng, {})
                if op.is_dma:
                    ck = {}
                    waits = []
                    for d in sorted(op.deps, key=lambda d_: -d_.idx):
                        if K.get(d.semkey, 0) >= d.semval:
                            continue
                        waits.append((d.semkey, d.semval))
                        for k_, v_ in clocks[d.idx].items():
                            if K.get(k_, 0) < v_:
                                K[k_] = v_
                    plan[op.idx] = waits
                    for d in op.deps:
                        for k_, v_ in clocks[d.idx].items():
                            if ck.get(k_, 0) < v_:
                                ck[k_] = v_
                    ck[op.semkey] = op.semval
                    clocks[op.idx] = ck
                    continue
                waits = []
                for d in sorted(op.deps, key=lambda d_: -d_.idx):
                    if K.get(d.semkey, 0) >= d.semval:
                        continue
                    waits.append((d.semkey, d.semval))
                    for k_, v_ in clocks[d.idx].items():
                        if K.get(k_, 0) < v_:
                            K[k_] = v_
                plan[op.idx] = waits
                ck = dict(K)
                if op.signal:
                    ck[op.semkey] = op.semval
                clocks[op.idx] = ck
            self.n_waits = {e: sum(len(plan[o.idx]) for o in ops_) for e, ops_ in per_eng.items()}

            def run(engname):
                def body(eng):
                    for op in per_eng.get(engname, ()):
                        waits = plan[op.idx]
                        if op.fn is None:
                            for k, v in waits:
                                eng.wait_ge(sems[k], v)
                            continue
                        for k, v in waits[:-1]:
                            eng.wait_ge(sems[k], v)
                        ret = op.fn(eng)
                        first, last = ret if isinstance(ret, tuple) else (ret, ret)
                        if waits:
                            k, v = waits[-1]
                            first._wait_ge(sems[k], v)
                        if op.signal:
                            last.then_inc(sems[op.semkey], 16 if op.is_dma else 1)
                return body

            block.tensor(run("pe"))
            block.scalar(run("act"))
            block.vector(run("dve"))
            block.gpsimd(run("pool"))
            block.sync(run("sp"))
        return counts


def build_nc(ntok=SEQ, stop_after=None, dbg=9):
    nmt = ntok // T
    nc = bass.Bass("TRN2", target_bir_lowering=False)
    x_d = nc.dram_tensor("x", [ntok, D], F32, kind="ExternalInput").ap()
    out_d = nc.dram_tensor("out", [ntok, D], F32, kind="ExternalOutput").ap()
    vecs_d = nc.dram_tensor("vecs", [128, NV], F32, kind="ExternalInput").ap()
    ident_d = nc.dram_tensor("ident", [128, 128], F32, kind="ExternalInput").ap()
    sident_d = nc.dram_tensor("sident", [128, 32], F32, kind="ExternalInput").ap()
    w_in_d = nc.dram_tensor("w_in_s", [8, 128, 3072], F32, kind="ExternalInput").ap()
    w_out_d = nc.dram_tensor("w_out_s", [4, 128, 2048], F32, kind="ExternalInput").ap()
    pw1_d = nc.dram_tensor("pw1_s", [8, 128, 2048], F32, kind="ExternalInput").ap()
    pw2_d = nc.dram_tensor("pw2_s", [4, 128, 2048], F32, kind="ExternalInput").ap()
    gu_d = [nc.dram_tensor("gu%d_s" % l, [FC, 128, 2048], F32, kind="ExternalInput").ap() for l in range(2)]
    wd_d = [nc.dram_tensor("wd%d_s" % l, [8, 128, HC * 256], F32, kind="ExternalInput").ap() for l in range(2)]

    A = nc.alloc_sbuf_tensor
    xb = [A("xb0", [128, KC, T], F32), A("xb1", [128, KC, T], F32)]
    xn = A("xn", [128, KC, T], BF16)
    cb = A("convbuf", [128, KC, CBW], BF16)
    cvh = A("cvh", [128, KC, K0 - 1], BF16)
    uh = A("uh", [128, KC, K1 - 1], BF16)
    h = A("h_arena", [128, HC, T], BF16)
    sq = A("sq", [128, KC, T], BF16)
    vbf = A("vbf", [128, KC, T], BF16)
    tmp = A("tmp", [128, NTMP, ST], F32)
    stat = A("stat", [128, 4, ST], F32)
    stw = A("stw", [128, 2, 32 * 32], BF16)
    sti = A("sti", [128, 32], BF16)
    stif = A("stif", [128, 32], F32)
    idf = A("idf", [128, 128], F32)
    idb = A("idb", [128, 128], BF16)
    ones = A("ones", [128, 128], BF16)
    vecs = A("vecs_sb", [128, NV], F32)
    wring = A("wring", [128, NSLOT, SLAB], BF16)
    RBW = CBW
    hflat = h[:, :, :].rearrange("p a b -> p (a b)")
    sq32 = sq.bitcast(F32)
    vbf32 = vbf.bitcast(F32)
    banks = [nc.alloc_psum_tensor("bank%d" % i, [128, 512], F32) for i in range(8)]

    S = Sched(nc)
    st = {"bank": 0, "tmp": 0, "dslot": 0, "cur": 0}

    def nbank():
        b = st["bank"]
        st["bank"] = (b + 1) % 8
        return b

    def ntmp():
        t_ = st["tmp"]
        st["tmp"] = (t_ + 1) % NTMP
        return t_

    def vcol(base, i):
        return vecs[:, base + i:base + i + 1]

    slabs = []

    def ffn_slabs(l):
        for hf in range(2):
            for fl in range(HC):
                slabs.append((gu_d[l][hf * HC + fl], 2048))
            for j in range(4):
                slabs.append((wd_d[l][hf * 4 + j], HC * 256))

    for mt in range(nmt):
        for c in range(8):
            slabs.append((w_in_d[c], 3072))
        for j in range(4):
            slabs.append((w_out_d[j], 2048))
        ffn_slabs(0)
        for c in range(8):
            slabs.append((pw1_d[c], 2048))
        for j in range(4):
            slabs.append((pw2_d[j], 2048))
        ffn_slabs(1)
    ws = {"loaded": 0, "cur": 0}

    def ws_advance(i):
        upto = min(len(slabs), i + NSLOT)
        while ws["loaded"] < upto:
            j = ws["loaded"]
            slot = j % NSLOT
            ap, n = slabs[j]
            S.add("pool", lambda e, slot=slot, ap=ap, n=n: e.dma_start(out=wring[:, slot, 0:n], in_=ap),
                  writes=[("w", slot)], dma_key=("w", slot))
            ws["loaded"] += 1

    def ws_hold(n):
        first = ws["cur"]
        ws_advance(first)
        ws["cur"] += n
        return first, [(first + i) % NSLOT for i in range(n)]

    S.add("sp", lambda e: e.dma_start(out=idf[:], in_=ident_d), writes=["idf"], dma_key="idf")
    S.add("sp", lambda e: e.dma_start(out=vecs[:], in_=vecs_d), writes=["vecs"], dma_key="vecs")
    S.add("dve", lambda e: e.memset(ones[:], 1.0), writes=["ones"])
    S.add("dve", lambda e: e.tensor_copy(out=idb[:], in_=idf[:]), reads=["idf"], writes=["idb"])
    S.add("dve", lambda e: e.memset(cvh[:], 0.0), writes=["cvh"])
    S.add("dve", lambda e: e.memset(uh[:], 0.0), writes=["uh"])
    S.add("dve", lambda e: e.memset(cb[:, :, :], 0.0),
          writes=["cbpad"] + [("cbh", c) for c in range(KC)] + [("cb", c, s_) for c in range(KC) for s_ in range(NS)])
    S.add("sp", lambda e: e.dma_start(out=stif[:], in_=sident_d), writes=["stif"], dma_key="stif")
    S.add("dve", lambda e: e.tensor_copy(out=sti[:], in_=stif[:]), reads=["stif"], writes=["sti"])
    ws_advance(0)

    def tok(s):
        return slice(s * ST, (s + 1) * ST)

    pend = []

    def defer(n, tag, fn):
        pend.append([n, tag, fn])

    def pe_tick():
        due = []
        for p in pend:
            p[0] -= 1
            if p[0] <= 0:
                due.append(p)
        for p in due:
            pend.remove(p)
        for p in due:
            p[2]()

    def need(tag):
        due = [p for p in pend if p[1] == tag]
        for p in due:
            pend.remove(p)
        for p in due:
            p[2]()

    def mm_multi(groups, reads, tick=True):
        def fn(e):
            first = ins = None
            for bank, pairs in groups:
                n = len(pairs)
                for i, (l, r) in enumerate(pairs):
                    ins = e.matmul(banks[bank][:, 0:512], lhsT=l, rhs=r, start=(i == 0), stop=(i == n - 1))
                    if first is None:
                        first = ins
            return first, ins
        S.add("pe", fn, reads=reads, writes=[("ps", g[0]) for g in groups])
        if tick:
            pe_tick()

    def mm_group(bank, pairs, reads, tick=True):
        mm_multi([(bank, pairs)], reads, tick)

    def start_norm(s, gbase, to_x, par=None, delay=1):
        p = st["cur"] if par is None else par
        xs = xb[p]
        dst = xs if to_x else xn
        dkey = (lambda c: ("x", p, s, c)) if to_x else (lambda c: ("xn", s, c))
        for c in range(KC):
            S.add("act", lambda e, c=c: e.activation(out=sq[:, c, tok(s)], in_=xs[:, c, tok(s)], func=AF.Square),
                  reads=[("x", p, s, c)], writes=[("sq", s, c)])

        def rest():
            b = nbank()
            mm_group(b, [(ones[:], sq[:, c, tok(s)]) for c in range(KC)],
                     reads=["ones"] + [("sq", s, c) for c in range(KC)], tick=False)
            S.add("act", lambda e: e.activation(out=stat[:, s, :], in_=banks[b][:, 0:512], func=AF.Ln,
                                                scale=1.0 / D, bias=RMS_EPS),
                  reads=[("ps", b)], writes=[("stat", s)])
            S.add("act", lambda e: e.activation(out=stat[:, s, :], in_=stat[:, s, :], func=AF.Exp, scale=-0.5),
                  reads=[("stat", s)], writes=[("stat", s)])
            for c in range(KC):
                S.add("dve", lambda e, c=c: e.scalar_tensor_tensor(
                    out=dst[:, c, tok(s)], in0=xs[:, c, tok(s)], scalar=vcol(gbase, c), in1=stat[:, s, :],
                    op0=ALU.mult, op1=ALU.mult),
                    reads=[("x", p, s, c), ("stat", s), "vecs"], writes=[dkey(c)])
        defer(delay, ("norm", s), rest)

    def resid_add(bank, s, m, bias_col=None):
        p = st["cur"]
        xs = xb[p]
        if bias_col is None:
            S.add("dve", lambda e: e.tensor_tensor(out=xs[:, m, tok(s)], in0=banks[bank][:, 0:512],
                                                   in1=xs[:, m, tok(s)], op=ALU.add),
                  reads=[("ps", bank), ("x", p, s, m)], writes=[("x", p, s, m)])
        else:
            S.add("dve", lambda e: e.scalar_tensor_tensor(out=xs[:, m, tok(s)], in0=banks[bank][:, 0:512],
                                                          scalar=bias_col, in1=xs[:, m, tok(s)],
                                                          op0=ALU.add, op1=ALU.add),
                  reads=[("ps", bank), ("x", p, s, m), "vecs"], writes=[("x", p, s, m)])

    def proj_out(src, skey, nk, nxt, bias_base=None, kouter0=False):
        first, slots = ws_hold(4)
        for s in range(NS):
            if kouter0 and s == 0:
                for m0 in (0, 4):
                    bks = [nbank() for _ in range(4)]
                    for kc in range(nk):
                        def fn(e, kc=kc, m0=m0, bks=bks, s=s):
                            first_ = ins = None
                            for i in range(4):
                                m = m0 + i
                                slot, mm = slots[m // 2], m % 2
                                ins = e.matmul(banks[bks[i]][:, 0:512],
                                               lhsT=wring[:, slot, kc * 256 + mm * 128: kc * 256 + mm * 128 + 128],
                                               rhs=src[:, kc, tok(s)], start=(kc == 0), stop=(kc == nk - 1))
                                if first_ is None:
                                    first_ = ins
                            return first_, ins
                        S.add("pe", fn, reads=[("w", sl) for sl in slots] + [(skey, s, kc)],
                              writes=[("ps", b) for b in bks])
                        pe_tick()
                    for i in range(4):
                        m = m0 + i
                        resid_add(bks[i], s, m, None if bias_base is None else vcol(bias_base, m))
                if nxt is not None:
                    start_norm(s, *nxt)
                continue
            for j in range(4):
                slot = slots[j]
                groups = []
                for mm in range(2):
                    b = nbank()
                    groups.append((b, [(wring[:, slot, kc * 256 + mm * 128: kc * 256 + mm * 128 + 128],
                                        src[:, kc, tok(s)]) for kc in range(nk)]))
                mm_multi(groups, [("w", slot)] + [(skey, s, kc) for kc in range(nk)])
                for mm in range(2):
                    m = 2 * j + mm
                    resid_add(groups[mm][0], s, m, None if bias_base is None else vcol(bias_base, m))
            if nxt is not None:
                start_norm(s, *nxt)
        ws_advance(first + 4)

    def blocks(n, g=3):
        out, i = [], 0
        while i < n:
            out.append(list(range(i, min(n, i + g))))
            i += g
        return out

    def ffn(l, nxt, next_in=None):
        for hf in range(2):
            for bix, blk in enumerate(blocks(HC)):
                if next_in is not None and hf == 0 and bix == 1:
                    next_in("schedule")
                first, slots = ws_hold(len(blk))
                for s in range(NS):
                    need(("norm", s))
                    for bi, fl in enumerate(blk):
                        slot = slots[bi]
                        bg, bu = nbank(), nbank()
                        groups = [(b, [(wring[:, slot, (kc * 2 + j) * 128:(kc * 2 + j + 1) * 128], xn[:, kc, tok(s)])
                                       for kc in range(KC)]) for j, b in ((0, bg), (1, bu))]
                        mm_multi(groups, [("w", slot)] + [("xn", s, kc) for kc in range(KC)])
                        t_ = ntmp()
                        S.add("act", lambda e, t_=t_, bg=bg: e.activation(out=tmp[:, t_, :], in_=banks[bg][:, 0:512],
                                                                          func=AF.Silu),
                              reads=[("ps", bg)], writes=[("tmp", t_)])
                        S.add("dve", lambda e, t_=t_, bu=bu, fl=fl, s=s: e.tensor_tensor(
                            out=h[:, fl, tok(s)], in0=banks[bu][:, 0:512], in1=tmp[:, t_, :], op=ALU.mult),
                            reads=[("ps", bu), ("tmp", t_)], writes=[("h", s, fl)])
                ws_advance(first + len(blk))
            if next_in is not None and hf == 1:
                next_in("finish")
            proj_out(h, "h", HC, nxt if hf == 1 else None)

    def mixer0(nxt):
        S.add("pool", lambda e: e.tensor_copy(out=cb[:, :, OFF - (K0 - 1):OFF], in_=cvh[:, :, :]),
              reads=["cvh"], writes=[("cbh", c) for c in range(KC)])
        for blk in blocks(KC):
            first, slots = ws_hold(len(blk))
            for s in range(NS):
                need(("norm", s))
                for bi, c in enumerate(blk):
                    slot = slots[bi]
                    bB, bC, bV = nbank(), nbank(), nbank()
                    if bi == 0 and blk[0] == 0:
                        for kc in range(KC):
                            def fn(e, kc=kc, slot=slot, s=s, bb=(bC, bV, bB)):
                                first_ = ins = None
                                for j, b in ((1, bb[0]), (2, bb[1]), (0, bb[2])):
                                    ins = e.matmul(banks[b][:, 0:512],
                                                   lhsT=wring[:, slot, (kc * 3 + j) * 128:(kc * 3 + j + 1) * 128],
                                                   rhs=xn[:, kc, tok(s)], start=(kc == 0), stop=(kc == KC - 1))
                                    if first_ is None:
                                        first_ = ins
                                return first_, ins
                            S.add("pe", fn, reads=[("w", slot), ("xn", s, kc)],
                                  writes=[("ps", bC), ("ps", bV), ("ps", bB)])
                        pe_tick()
                    else:
                        groups = [(b, [(wring[:, slot, (kc * 3 + j) * 128:(kc * 3 + j + 1) * 128], xn[:, kc, tok(s)])
                                       for kc in range(KC)]) for j, b in ((1, bC), (2, bV), (0, bB))]
                        mm_multi(groups, [("w", slot)] + [("xn", s, kc) for kc in range(KC)])
                    tc_, tb_, ta_ = ntmp(), ntmp(), ntmp()
                    S.add("act", lambda e, tc_=tc_, bC=bC: e.activation(out=tmp[:, tc_, :], in_=banks[bC][:, 0:512],
                                                                        func=AF.Copy),
                          reads=[("ps", bC)], writes=[("tmp", tc_)])
                    S.add("act", lambda e, tb_=tb_, bB=bB: e.activation(out=tmp[:, tb_, :], in_=banks[bB][:, 0:512],
                                                                        func=AF.Copy),
                          reads=[("ps", bB)], writes=[("tmp", tb_)])
                    S.add("dve", lambda e, tc_=tc_, bV=bV, c=c, s=s: e.tensor_tensor(
                        out=cb[:, c, OFF + s * ST:OFF + (s + 1) * ST], in0=banks[bV][:, 0:512], in1=tmp[:, tc_, :],
                        op=ALU.mult),
                        reads=[("ps", bV), ("tmp", tc_)], writes=[("cb", c, s)])
                    t0 = OFF + s * ST - (K0 - 1)
                    creads = [("cb", c, s), "vecs"] + ([("cbh", c)] if s == 0 else [("cb", c, 0)])
                    S.add("dve", lambda e, ta_=ta_, c=c, t0=t0: e.tensor_scalar(
                        out=tmp[:, ta_, :], in0=cb[:, c, t0:t0 + ST], scalar1=vcol(V_ACONV, c), scalar2=None,
                        op0=ALU.mult), reads=creads, writes=[("tmp", ta_)])
                    for k in (1, 2):
                        S.add("dve", lambda e, ta_=ta_, c=c, t0=t0, k=k: e.scalar_tensor_tensor(
                            out=tmp[:, ta_, :], in0=cb[:, c, t0 + k:t0 + k + ST], scalar=vcol(V_ACONV, k * 8 + c),
                            in1=tmp[:, ta_, :], op0=ALU.mult, op1=ALU.add),
                            reads=creads + [("tmp", ta_)], writes=[("tmp", ta_)])
                    S.add("dve", lambda e, ta_=ta_, tb_=tb_, c=c, s=s: e.tensor_tensor(
                        out=h[:, c, tok(s)], in0=tmp[:, ta_, :], in1=tmp[:, tb_, :], op=ALU.mult),
                        reads=[("tmp", ta_), ("tmp", tb_)], writes=[("h", s, c)])
            ws_advance(first + len(blk))
        S.add("pool", lambda e: e.tensor_copy(out=cvh[:, :, :], in_=cb[:, :, OFF + T - (K0 - 1):OFF + T]),
              reads=[("cb", c, 1) for c in range(KC)], writes=["cvh"])
        proj_out(h, "h", KC, nxt)

    def mixer1(nxt):
        pv = 1 - st["cur"]
        vf = xb[pv]
        S.add("pool", lambda e: e.tensor_copy(out=cb[:, :, OFF - (K1 - 1):OFF], in_=uh[:, :, :]),
              reads=["uh"], writes=[("cbh", c) for c in range(KC)])

        RB0 = [0, 5 * T]

        def rb_keys(buf):
            lo, hi = RB0[buf], RB0[buf] + 4 * RBW - 1
            return [("h", s_, fl) for fl in range(lo // T, hi // T + 1) for s_ in range(NS)]

        def replicate(c):
            buf = c % 2
            for g in range(4):
                for j in range(4):
                    o0 = RB0[buf] + g * RBW
                    q_ = "sp" if g < (2 if c in (0, KC - 1) else 3) else "act"
                    S.add(q_, lambda e, g=g, j=j, o0=o0: e.dma_start(
                        out=hflat[32 * j:32 * j + 32, o0:o0 + RBW - 4], in_=cb[32 * g:32 * g + 32, c, j:j + RBW - 4]),
                        reads=[("cb", c, 0), ("cb", c, 1), ("cbh", c), "cbpad"],
                        writes=[("rb", buf, g, j)] + (rb_keys(buf) if (g == 0 and j == 0) else []),
                        dma_key=("rb", buf, q_))

        def build_stw(c):
            buf = c % 2

            def fn(e):
                first = ins = None
                for g in range(4):
                    for k0 in range(8):
                        i = g * 8 + k0
                        ins = e.tensor_scalar(out=stw[:, buf, i * 32:(i + 1) * 32], in0=sti[:],
                                              scalar1=vcol(V_WST, (c * 4 + g) * 8 + k0), scalar2=None, op0=ALU.mult)
                        if first is None:
                            first = ins
                return first, ins
            S.add("dve", fn, reads=["sti", "vecs"], writes=[("stw", buf)])

        def conv_pe(c, s):
            buf = c % 2
            bV = nbank()
            base = OFF + s * ST - (K1 - 1)

            def fn(e):
                first = ins = None
                for k0 in range(8):
                    for g in range(4):
                        i = g * 8 + k0
                        o0 = RB0[buf] + g * RBW + base + 4 * k0
                        ins = e.matmul(banks[bV][32 * g:32 * g + 32, 0:512], lhsT=stw[:, buf, i * 32:(i + 1) * 32],
                                       rhs=hflat[:, o0:o0 + ST], start=(k0 == 0), stop=(k0 == 7),
                                       tile_position=(0, 32 * g))
                        if first is None:
                            first = ins
                return first, ins
            S.add("pe", fn, reads=[("rb", buf, g, j) for g in range(4) for j in range(4)] + [("stw", buf)] + rb_keys(buf),
                  writes=[("ps", bV)])
            pe_tick()
            return bV

        def conv_evac(bV, c, s):
            S.add("act", lambda e: e.activation(out=vbf[:, c, tok(s)], in_=banks[bV][:, 0:512],
                                                func=AF.Identity, bias=vcol(V_BDW, c)),
                  reads=[("ps", bV), "vecs"], writes=[("vbf", s, c)])
            S.add("act", lambda e: e.activation(out=sq[:, c, tok(s)], in_=banks[bV][:, 0:512],
                                                func=AF.Square, bias=vcol(V_BDW, c)),
                  reads=[("ps", bV), "vecs"], writes=[("sq", s, c)])
            S.add("act", lambda e: e.activation(out=vf[:, c, tok(s)], in_=banks[bV][:, 0:512],
                                                func=AF.Identity, bias=vcol(V_BDW, c)),
                  reads=[("ps", bV), "vecs"], writes=[("x", pv, s, c)])

        def pw1_item(c, s, slot):
            bA, bG = nbank(), nbank()
            groups = [(b, [(wring[:, slot, (kc * 2 + j) * 128:(kc * 2 + j + 1) * 128], xn[:, kc, tok(s)])
                           for kc in range(KC)]) for j, b in ((1, bG), (0, bA))]
            mm_multi(groups, [("w", slot)] + [("xn", s, kc) for kc in range(KC)])
            t_ = ntmp()
            S.add("act", lambda e: e.activation(
                out=tmp[:, t_, :], in_=banks[bG][:, 0:512], func=AF.Sigmoid, bias=vcol(V_BPW1, 8 + c)),
                reads=[("ps", bG), "vecs"], writes=[("tmp", t_)])
            S.add("dve", lambda e: e.scalar_tensor_tensor(
                out=cb[:, c, OFF + s * ST:OFF + (s + 1) * ST], in0=banks[bA][:, 0:512],
                scalar=vcol(V_BPW1, c), in1=tmp[:, t_, :], op0=ALU.add, op1=ALU.mult),
                reads=[("ps", bA), ("tmp", t_), "vecs"], writes=[("cb", c, s)])

        def conv_chunk(c):
            for s in range(NS):
                bV = conv_pe(c, s)
                conv_evac(bV, c, s)

        full_mix = dbg >= 2
        first, slots = ws_hold(2)
        for s in range(NS):
            need(("norm", s))
            for c in (0, 1):
                pw1_item(c, s, slots[c])
                if full_mix and s == NS - 1 and c == 0:
                    replicate(c)
        ws_advance(first + 2)
        if full_mix:
            build_stw(0)
        for c in range(2, KC):
            first, slots = ws_hold(1)
            for s in range(NS):
                pw1_item(c, s, slots[0])
                if full_mix and c == 3 and s == 0:
                    conv_chunk(0)
                    replicate(2)
            ws_advance(first + 1)
            if full_mix:
                if c == 2:
                    replicate(1)
                    build_stw(1)
                    continue
                if c == 3:
                    conv_chunk(1)
                    replicate(3)
                    build_stw(2)
                    continue
                conv_chunk(c - 2)
                replicate(c)
                build_stw(c - 1)
        if not full_mix:
            return
        conv_chunk(KC - 2)
        build_stw(KC - 1)
        conv_chunk(KC - 1)

        def ln_stats(s):
            b1, b2 = nbank(), nbank()
            mm_group(b1, [(ones[:], vbf[:, c, tok(s)]) for c in range(KC)],
                     ["ones"] + [("vbf", s, c) for c in range(KC)], tick=False)
            mm_group(b2, [(ones[:], sq[:, c, tok(s)]) for c in range(KC)],
                     ["ones"] + [("sq", s, c) for c in range(KC)], tick=False)
            return b1, b2

        def ln_chain(s, b1, b2, eng):
            m_, r_ = 2 - 2 * s, 3 - 2 * s
            S.add("act", lambda e: e.activation(out=stat[:, m_, :], in_=banks[b1][:, 0:512],
                                                func=AF.Copy, scale=1.0 / D),
                  reads=[("ps", b1)], writes=[("stat", m_)])
            S.add("act", lambda e: e.activation(out=stat[:, r_, :], in_=banks[b1][:, 0:512],
                                                func=AF.Square, scale=1.0 / D),
                  reads=[("ps", b1)], writes=[("stat", r_)])
            S.add("dve", lambda e: e.scalar_tensor_tensor(out=stat[:, r_, :], in0=banks[b2][:, 0:512],
                                                          scalar=1.0 / D, in1=stat[:, r_, :],
                                                          op0=ALU.mult, op1=ALU.subtract),
                  reads=[("ps", b2), ("stat", r_)], writes=[("stat", r_)])
            S.add("act", lambda e: e.activation(out=stat[:, r_, :], in_=stat[:, r_, :], func=AF.Ln,
                                                scale=1.0, bias=LN_EPS),
                  reads=[("stat", r_)], writes=[("stat", r_)])
            S.add("act", lambda e: e.activation(out=stat[:, r_, :], in_=stat[:, r_, :], func=AF.Exp, scale=-0.5),
                  reads=[("stat", r_)], writes=[("stat", r_)])
            S.add(eng, lambda e: e.tensor_tensor(out=stat[:, m_, :], in0=stat[:, m_, :], in1=stat[:, r_, :],
                                                 op=ALU.mult),
                  reads=[("stat", m_), ("stat", r_)], writes=[("stat", m_)])

        def ln_apply(c, s):
            m_, r_ = 2 - 2 * s, 3 - 2 * s
            S.add("dve", lambda e: e.tensor_tensor(out=vf[:, c, tok(s)], in0=vf[:, c, tok(s)], in1=stat[:, r_, :],
                                                   op=ALU.mult),
                  reads=[("x", pv, s, c), ("stat", r_)], writes=[("x", pv, s, c)])
            S.add("dve", lambda e: e.tensor_tensor(out=vf[:, c, tok(s)], in0=vf[:, c, tok(s)], in1=stat[:, m_, :],
                                                   op=ALU.subtract),
                  reads=[("x", pv, s, c), ("stat", m_)], writes=[("x", pv, s, c)])
            S.add("act", lambda e: e.activation(out=xn[:, c, tok(s)], in_=vf[:, c, tok(s)], func=AF.Silu,
                                                scale=vcol(V_LNG, c), bias=vcol(V_LNB, c)),
                  reads=[("x", pv, s, c), "vecs"], writes=[("xn", s, c)])

        S.add("pool", lambda e: e.tensor_copy(out=uh[:, :, :], in_=cb[:, :, OFF + T - (K1 - 1):OFF + T]),
              reads=[("cb", c, 1) for c in range(KC)], writes=["uh"])
        st0 = ln_stats(0)
        st1 = ln_stats(1)
        ln_chain(0, *st0, eng="dve")
        ln_chain(1, *st1, eng="pool")
        for s in range(NS):
            for c in range(KC):
                ln_apply(c, s)
        proj_out(xn, "xn", KC, nxt, bias_base=V_BPW2, kouter0=True)

    NSIN = 4
    NSOUT = 4
    in_issued = set()

    def stage_of(mt, tb):
        return (vbf32, "vbf") if (mt == 0 and tb >= NSIN) else (sq32, "sq")

    def issue_input(mt, tb):
        if (mt, tb) in in_issued or mt >= nmt or tb >= T // 128:
            return
        in_issued.add((mt, tb))
        sl_ = tb % NSIN
        r0 = mt * T + tb * 128
        stg, sk = stage_of(mt, tb)
        S.add("sp", lambda e: e.dma_start(out=stg[:, 2 * sl_:2 * sl_ + 2, :],
                                          in_=x_d[r0:r0 + 128, :].rearrange("p (a b) -> p a b", a=2)),
              writes=[(sk, s_, 2 * sl_ + a_) for s_ in range(NS) for a_ in range(2)], dma_key=("sin", sk, sl_))

    def load_tb(mt, tb, par, nxt, tick):
        sl_ = tb % NSIN
        s = tb // 4
        xs = xb[par]
        issue_input(mt, tb)
        stg, sk = stage_of(mt, tb)
        for half in range(2):
            b = nbank()
            ch = 2 * sl_ + half

            def fn(e, b=b, ch=ch):
                first = ins = None
                for q in range(4):
                    ins = e.transpose(out=banks[b][:, q * 128:(q + 1) * 128],
                                      in_=stg[:, ch, q * 128:(q + 1) * 128], identity=idf[:])
                    if first is None:
                        first = ins
                return first, ins
            S.add("pe", fn, reads=[(sk, s_, ch) for s_ in range(NS)] + ["idf"], writes=[("ps", b)])
            if tick:
                pe_tick()
            c0 = half * 4
            t0 = tb * 128
            src = banks[b][:, 0:512].rearrange("p (q t) -> p q t", q=4)
            wk = [("x", par, s, c0 + q) for q in range(4)]
            if half == 0:
                S.add("dve", lambda e, c0=c0, t0=t0, src=src: e.tensor_copy(
                    out=xs[:, c0:c0 + 4, t0:t0 + 128], in_=src), reads=[("ps", b)], writes=wk)
            else:
                S.add("act", lambda e, c0=c0, t0=t0, src=src: e.activation(
                    out=xs[:, c0:c0 + 4, t0:t0 + 128], in_=src, func=AF.Copy), reads=[("ps", b)], writes=wk)
        issue_input(mt, tb + NSIN)
        if tb % 4 == 3 and nxt is not None:
            start_norm(s, *nxt, par=par)

    out_ops = {}

    def store_tb(mt, tb, par, tick):
        sl_ = tb % NSOUT
        s = tb // 4
        xs = xb[par]
        need(("norm", s))
        r0 = mt * T + tb * 128
        for half in range(2):
            b = nbank()

            def fn(e, b=b, half=half):
                first = ins = None
                for q in range(4):
                    c = half * 4 + q
                    ins = e.transpose(out=banks[b][:, q * 128:(q + 1) * 128],
                                      in_=xs[:, c, tb * 128:(tb + 1) * 128], identity=idf[:])
                    if first is None:
                        first = ins
                return first, ins
            S.add("pe", fn, reads=[("x", par, s, half * 4 + q) for q in range(4)] + ["idf"], writes=[("ps", b)])
            if tick:
                pe_tick()
            ch = 2 * sl_ + half
            S.add("act", lambda e, b=b, ch=ch: e.activation(
                out=vbf32[:, ch, :], in_=banks[b][:, 0:512], func=AF.Copy),
                reads=[("ps", b)], writes=[("vbf", 0, ch), ("vbf", 1, ch)])
        hk = [("vbf", s_, 2 * sl_ + hf_) for s_ in range(2) for hf_ in range(2)]
        out_ops[sl_] = S.add("sp", lambda e: e.dma_start(
            out=out_d[r0:r0 + 128, :].rearrange("p (a b) -> p a b", a=2), in_=vbf32[:, 2 * sl_:2 * sl_ + 2, :]),
            reads=hk, dma_key=("sout", sl_))

    def flush(tag):
        need(tag)

    N_A = (V_ANORM, False)
    N_B = (V_BNORM, False)
    N_F0 = (V_FNORM0, False)
    N_F1 = (V_FNORM1, False)
    N_FIN = (V_FINAL, True)
    order = ["in", "l0m", "l0f", "l1m", "l1f"]
    nph = len(order) if stop_after is None else order.index(stop_after) + 1
    n_slab_ph = {"in": 0, "l0m": 12, "l0f": 30, "l1m": 12, "l1f": 30}
    full = stop_after is None
    NTB = T // 128
    if full:
        for tb in range(NTB):
            issue_input(0, tb)
        for tb in range(NTB):
            load_tb(0, tb, 0, N_A, True)
        for mt in range(nmt):
            cur = mt % 2
            st["cur"] = cur
            mixer0(N_F0)
            flush(("out",))
            ffn(0, N_B)
            mixer1(N_F1)

            def next_in(what, mt=mt, cur=cur):
                if mt + 1 >= nmt:
                    return
                if what == "schedule":
                    for tb in range(NSIN):
                        issue_input(mt + 1, tb)
                    for tb in range(NTB):
                        defer(2 + 2 * tb, ("in",), lambda tb=tb: load_tb(mt + 1, tb, 1 - cur, None, False))
                else:
                    flush(("in",))
                    for s_ in range(NS):
                        start_norm(s_, V_ANORM, False, par=1 - cur)
            ffn(1, N_FIN, next_in)
            for tb in range(NTB):
                if mt + 1 < nmt:
                    defer(2 + 2 * tb, ("out",), lambda tb=tb, mt=mt, cur=cur: store_tb(mt, tb, cur, False))
                else:
                    store_tb(mt, tb, cur, True)
    else:
        for mt in range(nmt):
            st["cur"] = 0
            for tb in range(NTB):
                load_tb(mt, tb, 0, None, True)
            if nph > 1:
                for s_ in range(NS):
                    start_norm(s_, *N_A, par=0)
            for ph in order[:nph]:
                if ph == "l0m":
                    mixer0(N_F0 if nph > 2 else None)
                elif ph == "l0f":
                    ffn(0, N_B if nph > 3 else None)
                elif ph == "l1m":
                    mixer1(N_F1 if nph > 4 else None)
                elif ph == "l1f":
                    ffn(1, None)
            for ph in order[nph:]:
                ws["cur"] += n_slab_ph[ph]
            for tb in range(NTB):
                store_tb(mt, tb, 0, True)
    assert not pend
    S.emit(final_wait_ops=list(out_ops.values()))
    return nc


def _fm(v):
    v = np.asarray(v, dtype=np.float32)
    return np.ascontiguousarray(v.reshape(-1, 128).T)


def prep_weights(a_norm, a_w_in, a_conv, a_w_out, b_norm, b_w_pw1, b_b_pw1, b_conv, b_b_conv, b_ln_g, b_ln_b,
                 b_w_pw2, b_b_pw2, ffn_norm, ffn_w_gate, ffn_w_up, ffn_w_down, final_norm):
    f32 = lambda a: np.asarray(a, dtype=np.float32)
    vecs = np.zeros((128, NV), np.float32)
    vecs[:, V_ANORM:V_ANORM + 8] = _fm(f32(a_norm)[0])
    vecs[:, V_BNORM:V_BNORM + 8] = _fm(f32(b_norm)[0])
    vecs[:, V_FNORM0:V_FNORM0 + 8] = _fm(f32(ffn_norm)[0])
    vecs[:, V_FNORM1:V_FNORM1 + 8] = _fm(f32(ffn_norm)[1])
    vecs[:, V_FINAL:V_FINAL + 8] = _fm(f32(final_norm))
    vecs[:, V_BPW1:V_BPW1 + 16] = _fm(f32(b_b_pw1)[0])
    vecs[:, V_BDW:V_BDW + 8] = _fm(f32(b_b_conv)[0])
    vecs[:, V_LNG:V_LNG + 8] = _fm(f32(b_ln_g)[0])
    vecs[:, V_LNB:V_LNB + 8] = _fm(f32(b_ln_b)[0])
    vecs[:, V_BPW2:V_BPW2 + 8] = _fm(f32(b_b_pw2)[0])
    vecs[:, V_ACONV:V_ACONV + 24] = _fm(f32(a_conv)[0].reshape(-1))
    bc = np.concatenate([f32(b_conv)[0], np.zeros((1, D), np.float32)], axis=0)
    bc = bc.reshape(8, 4, 8, 4, 32)
    vecs[:, V_WST:V_WST + 256] = np.ascontiguousarray(bc.transpose(1, 4, 2, 3, 0)).reshape(128, 256)

    def colgroup(w, ncol_groups, j_parts, gcols):
        din = w.shape[0]
        kc = din // 128
        w5 = w.reshape(kc, 128, j_parts, ncol_groups, gcols)
        return np.ascontiguousarray(w5.transpose(3, 1, 0, 2, 4)).reshape(ncol_groups, 128, kc * j_parts * gcols)

    res = {
        "vecs": vecs,
        "ident": np.eye(128, dtype=np.float32),
        "sident": np.tile(np.eye(32, dtype=np.float32), (4, 1)),
        "w_in_s": colgroup(f32(a_w_in)[0], 8, 3, 128),
        "w_out_s": colgroup(f32(a_w_out)[0], 4, 1, 256),
        "pw1_s": colgroup(f32(b_w_pw1)[0], 8, 2, 128),
        "pw2_s": colgroup(f32(b_w_pw2)[0], 4, 1, 256),
    }
    for l in range(2):
        gu = np.concatenate([f32(ffn_w_gate)[l], f32(ffn_w_up)[l]], axis=1)
        res["gu%d_s" % l] = colgroup(gu, FC, 2, 128)
        wdl = f32(ffn_w_down)[l]
        res["wd%d_s" % l] = np.concatenate([colgroup(wdl[hf * HC * 128:(hf + 1) * HC * 128], 4, 1, 256)
                                            for hf in range(2)], axis=0)
    return res


_NC_CACHE = {}


def kernel(x, **weights):
    x = np.asarray(x, dtype=np.float32)
    wmap = prep_weights(**weights)
    nb = x.shape[0]
    if "nc" not in _NC_CACHE:
        _NC_CACHE["nc"] = build_nc(SEQ)
    nc = _NC_CACHE["nc"]
    in_maps = []
    for b in range(nb):
        m = dict(wmap)
        m["x"] = np.ascontiguousarray(x[b])
        in_maps.append(m)
    res = run_bass_kernel_spmd(nc, in_maps, core_ids=list(range(nb)))
    return np.stack([np.asarray(r["out"], dtype=np.float32) for r in res.results], axis=0)
```
